# Optimizing a Trainium2 kernel written in Bass

```python
import jax, jax.numpy as jnp
from jax import lax
import numpy as np

D_MODEL = 1024
BATCH = 2
SEQ = 8192
DEPTH = 2

D_MIX = D_MODEL
EPS = 1e-6
D_A = D_MIX // 2
HD_A = 64
H_A = D_A // HD_A
DILATED_CFGS = ((128, 1), (512, 4), (2048, 16))
Q_BLOCK = 128
NUM_BUCKETS = 32
REL_MAX_DIST = 1024
D_B = D_MIX // 4
HD_B = 64
H_B = D_B // HD_B
MLSTM_CHUNK = 64
D_C = D_MIX - D_A - D_B
HD_C = 64
H_C = D_C // HD_C
GDN_CHUNK = 64
CONV_W = 5
_SIZES = (D_A, D_A, D_A, D_A,
          D_B, D_B, D_B, 4 * H_B, D_B, D_B,
          3 * D_C, 4 * H_C, D_C)
N_IN = sum(_SIZES)

kernel_name = 'hybrid_dilated_mlstm_gdn_encoder'


def _rmsnorm(t, w):
    tf = t.astype(jnp.float32)
    tf = tf * lax.rsqrt(jnp.mean(tf * tf, axis=-1, keepdims=True) + EPS)
    return (tf * w.astype(jnp.float32)).astype(t.dtype)


def _l2norm(t):
    return t * lax.rsqrt(jnp.sum(t * t, axis=-1, keepdims=True) + EPS)


def _heads(t, n_heads):
    b, s, w = t.shape
    return t.reshape(b, s, n_heads, w // n_heads).transpose(0, 2, 1, 3)


def _merge(t):
    b, h, s, d = t.shape
    return t.transpose(0, 2, 1, 3).reshape(b, s, h * d)


def _gates(t, n_heads):
    b, s, _ = t.shape
    return t.reshape(b, s, 4, n_heads).transpose(2, 0, 3, 1).astype(jnp.float32)


def _rev(t):
    return jnp.flip(t, axis=2)


def _t5_bucket(rel):
    half = NUM_BUCKETS // 2
    max_exact = half // 2
    n = np.abs(rel)
    large = max_exact + (np.log(np.maximum(n, 1) / max_exact) / np.log(REL_MAX_DIST / max_exact)
                         * (half - max_exact)).astype(np.int32)
    large = np.minimum(large, half - 1)
    return (rel > 0).astype(np.int32) * half + np.where(n < max_exact, n, large)


def _dilated_attention(q, k, v, rel_bias):
    s_len = q.shape[2]
    cfgs = []
    for (window, dil) in DILATED_CFGS:
        n_side = (window // 2) // dil
        offs = dil * np.arange(-n_side, n_side + 1)
        bias = rel_bias[_t5_bucket(offs)].T.astype(jnp.float32)
        cfgs.append((jnp.asarray(offs, jnp.int32), bias))

    def block(nb):
        start = nb * Q_BLOCK
        pos = start + jnp.arange(Q_BLOCK)
        qb = lax.dynamic_slice_in_dim(q, start, Q_BLOCK, axis=2)
        outs, lses = [], []
        for offs, bias in cfgs:
            idx = pos[:, None] + offs[None, :]
            valid = (idx >= 0) & (idx < s_len)
            idx = jnp.clip(idx, 0, s_len - 1)
            kg = jnp.take(k, idx, axis=2)
            vg = jnp.take(v, idx, axis=2)
            logits = jnp.einsum('bhqd,bhqjd->bhqj', qb, kg) + bias[None, :, None, :]
            logits = jnp.where(valid[None, None], logits, -jnp.inf)
            lse = jax.nn.logsumexp(logits, axis=-1)
            p = jnp.exp(logits - lse[..., None])
            outs.append(jnp.einsum('bhqj,bhqjd->bhqd', p, vg))
            lses.append(lse)
        wts = jax.nn.softmax(jnp.stack(lses), axis=0)
        return jnp.einsum('cbhq,cbhqd->bhqd', wts, jnp.stack(outs))

    o = lax.map(block, jnp.arange(s_len // Q_BLOCK))
    return o.transpose(1, 2, 0, 3, 4).reshape(q.shape)


def _mlstm_chunked(q, k, v, log_i, log_f):
    b, h, s_len, d = q.shape
    L = MLSTM_CHUNK
    nc = s_len // L

    def chunks(t):
        return jnp.moveaxis(t.reshape(b, h, nc, L, *t.shape[3:]), 2, 0)

    xs = tuple(map(chunks, (q * d ** -0.5, k, v, log_i, log_f)))
    causal = jnp.tril(jnp.ones((L, L), bool))

    def step(carry, inp):
        c_st, n_st, m_st = carry
        qt, kt, vt, it, ft = inp
        bcum = jnp.cumsum(ft, axis=-1)
        dmat = jnp.where(causal, bcum[..., :, None] - bcum[..., None, :] + it[..., None, :], -jnp.inf)
        inter = bcum + m_st[..., None]
        m_t = jnp.maximum(jnp.max(dmat, axis=-1), inter)
        sc = jnp.einsum('bhtd,bhsd->bhts', qt, kt) * jnp.exp(dmat - m_t[..., None])
        iw = jnp.exp(inter - m_t)
        num = iw[..., None] * jnp.einsum('bhtd,bhde->bhte', qt, c_st) + jnp.einsum('bhts,bhse->bhte', sc, vt)
        den = iw * jnp.einsum('bhtd,bhd->bht', qt, n_st) + jnp.sum(sc, axis=-1)
        h_t = num / jnp.maximum(jnp.abs(den), jnp.exp(-m_t))[..., None]
        b_last = bcum[..., -1]
        w_log = b_last[..., None] - bcum + it
        m_new = jnp.maximum(b_last + m_st, jnp.max(w_log, axis=-1))
        ws = jnp.exp(w_log - m_new[..., None])
        dec = jnp.exp(b_last + m_st - m_new)
        c_st = dec[..., None, None] * c_st + jnp.einsum('bhs,bhsd,bhse->bhde', ws, kt, vt)
        n_st = dec[..., None] * n_st + jnp.einsum('bhs,bhsd->bhd', ws, kt)
        return (c_st, n_st, m_new), h_t

    init = (jnp.zeros((b, h, d, v.shape[-1]), jnp.float32), jnp.zeros((b, h, d), jnp.float32),
            jnp.zeros((b, h), jnp.float32))
    _, hs = lax.scan(step, init, xs)
    return jnp.moveaxis(hs, 0, 2).reshape(b, h, s_len, -1)


def _gated_delta_chunked(q, k, v, beta, g):
    b, h, s_len, dk = q.shape
    L = GDN_CHUNK
    nc = s_len // L

    def rs(t):
        return jnp.moveaxis(t.reshape(b, h, nc, L, *t.shape[3:]), 2, 0)

    q, k, v, beta, g = map(rs, (q, k, v, beta, g))
    gam = jnp.cumsum(g, axis=-1)
    ar = jnp.arange(L)
    incl = ar[:, None] >= ar[None, :]
    strict = ar[:, None] > ar[None, :]
    decay = jnp.exp(jnp.where(incl, gam[..., :, None] - gam[..., None, :], -jnp.inf))
    m = jnp.where(strict, beta[..., :, None] * jnp.einsum('nbhid,nbhjd->nbhij', k, k) * decay, 0.0)
    eye = jnp.eye(L, dtype=m.dtype)
    t_inv = lax.linalg.triangular_solve(m + eye, jnp.broadcast_to(eye, m.shape), left_side=True,
                                        lower=True, unit_diagonal=True)
    u = jnp.einsum('nbhij,nbhjd->nbhid', t_inv, v * beta[..., None])
    w = jnp.einsum('nbhij,nbhjd->nbhid', t_inv, k * (beta * jnp.exp(gam))[..., None])
    qg = q * jnp.exp(gam)[..., None]
    a_intra = jnp.einsum('nbhid,nbhjd->nbhij', q, k) * decay
    g_last = gam[..., -1]
    k_dec = k * jnp.exp(g_last[..., None] - gam)[..., None]

    def step(state, inp):
        u_c, w_c, qg_c, a_c, kd_c, gl_c = inp
        v_new = u_c - jnp.einsum('bhid,bhde->bhie', w_c, state)
        o = jnp.einsum('bhid,bhde->bhie', qg_c, state) + jnp.einsum('bhij,bhje->bhie', a_c, v_new)
        state = state * jnp.exp(gl_c)[..., None, None] + jnp.einsum('bhid,bhie->bhde', kd_c, v_new)
        return state, o

    s0 = jnp.zeros((b, h, dk, v.shape[-1]), jnp.float32)
    _, o = lax.scan(step, s0, (u, w, qg, a_intra, k_dec, g_last))
    return jnp.moveaxis(o, 0, 2).reshape(b, h, s_len, -1)


def _short_conv(u, w):
    ch = u.shape[-1]
    return lax.conv_general_dilated(u, w[:, None, :], window_strides=(1,),
                                    padding=[(CONV_W // 2, CONV_W // 2)],
                                    dimension_numbers=('NWC', 'WIO', 'NWC'), feature_group_count=ch)


def _hybrid_layer(x, norm_w, w_in, w_out, qk_norm_w, rel_bias, i_bias, f_bias, m_norm_w,
                  conv_w, a_log, dt_bias, g_norm_w):
    f32 = jnp.float32
    hn = _rmsnorm(x, norm_w)
    proj = hn @ w_in
    splits = [int(s) for s in np.cumsum(_SIZES)[:-1]]
    (a_q, a_k, a_v, a_z, b_q, b_k, b_v, b_if, b_o, b_z, c_qkv, c_ab, c_z) = jnp.split(proj, splits, axis=-1)

    qa = _rmsnorm(_heads(a_q, H_A).astype(f32), qk_norm_w[0]) * HD_A ** -0.5
    ka = _rmsnorm(_heads(a_k, H_A).astype(f32), qk_norm_w[1])
    va = _heads(a_v, H_A).astype(f32)
    y_a = _merge(_dilated_attention(qa, ka, va, rel_bias)) * jax.nn.silu(a_z.astype(f32))

    qb, kb, vb = (_heads(t, H_B).astype(f32) for t in (b_q, b_k, b_v))
    gb = _gates(b_if, H_B)
    log_i = gb[0:2] + i_bias.astype(f32)[:, None, :, None]
    log_f = jax.nn.log_sigmoid(gb[2:4] + f_bias.astype(f32)[:, None, :, None])
    h_fwd = _mlstm_chunked(qb, kb, vb, log_i[0], log_f[0])
    h_bwd = _rev(_mlstm_chunked(_rev(qb), _rev(kb), _rev(vb), _rev(log_i[1]), _rev(log_f[1])))
    h_b = jax.nn.sigmoid(_heads(b_o, H_B).astype(f32)) * (h_fwd + h_bwd)
    y_b = _merge(_rmsnorm(h_b, m_norm_w)) * jax.nn.silu(b_z.astype(f32))

    cqkv = jax.nn.silu(_short_conv(c_qkv.astype(f32), conv_w.astype(f32)))
    c_q, c_k, c_v = jnp.split(cqkv, 3, axis=-1)
    qc = _l2norm(_heads(c_q, H_C)) * HD_C ** -0.5
    kc = _l2norm(_heads(c_k, H_C))
    vc = _heads(c_v, H_C)
    gc = _gates(c_ab, H_C)
    g_log = -jnp.exp(a_log.astype(f32))[:, None, :, None] * jax.nn.softplus(
        gc[0:2] + dt_bias.astype(f32)[:, None, :, None])
    beta = jax.nn.sigmoid(gc[2:4])
    o_fwd = _gated_delta_chunked(qc, kc, vc, beta[0], g_log[0])
    o_bwd = _rev(_gated_delta_chunked(_rev(qc), _rev(kc), _rev(vc), _rev(beta[1]), _rev(g_log[1])))
    y_c = _merge(_rmsnorm(o_fwd + o_bwd, g_norm_w)) * jax.nn.silu(c_z.astype(f32))

    y = jnp.concatenate([y_a, y_b, y_c], axis=-1).astype(x.dtype) @ w_out
    return x + y


def setup_inputs(seed: int = 0) -> dict:
    key = jax.random.key(seed)
    ks = jax.random.split(key, 14)
    nrm = jax.random.normal
    x = nrm(ks[0], (BATCH, SEQ, D_MODEL), jnp.float32)
    norm_w = 1.0 + 0.02 * nrm(ks[1], (DEPTH, D_MODEL), jnp.float32)
    w_in = nrm(ks[2], (DEPTH, D_MODEL, N_IN), jnp.float32) * D_MODEL ** -0.5
    w_out = nrm(ks[3], (DEPTH, D_MIX, D_MODEL), jnp.float32) * D_MIX ** -0.5
    qk_norm_w = 1.0 + 0.02 * nrm(ks[4], (DEPTH, 2, HD_A), jnp.float32)
    rel_bias = 0.1 * nrm(ks[5], (NUM_BUCKETS, H_A), jnp.float32)
    mlstm_i_bias = 0.1 * nrm(ks[6], (DEPTH, 2, H_B), jnp.float32)
    mlstm_f_bias = 3.0 + 3.0 * jax.random.uniform(ks[7], (DEPTH, 2, H_B), jnp.float32)
    mlstm_norm_w = 1.0 + 0.02 * nrm(ks[8], (DEPTH, HD_B), jnp.float32)
    gdn_conv_w = nrm(ks[9], (DEPTH, CONV_W, 3 * D_C), jnp.float32) * CONV_W ** -0.5
    gdn_a_log = jnp.log(jax.random.uniform(ks[10], (DEPTH, 2, H_C), jnp.float32, 1.0, 16.0))
    dt = jnp.exp(jax.random.uniform(ks[11], (DEPTH, 2, H_C), jnp.float32, np.log(1e-3), np.log(1e-1)))
    gdn_dt_bias = dt + jnp.log(-jnp.expm1(-dt))
    gdn_norm_w = 1.0 + 0.02 * nrm(ks[12], (DEPTH, HD_C), jnp.float32)
    return {'x': x, 'norm_w': norm_w, 'w_in': w_in, 'w_out': w_out, 'qk_norm_w': qk_norm_w,
            'rel_bias': rel_bias, 'mlstm_i_bias': mlstm_i_bias, 'mlstm_f_bias': mlstm_f_bias,
            'mlstm_norm_w': mlstm_norm_w, 'gdn_conv_w': gdn_conv_w, 'gdn_a_log': gdn_a_log,
            'gdn_dt_bias': gdn_dt_bias, 'gdn_norm_w': gdn_norm_w}


def reference(x, norm_w, w_in, w_out, qk_norm_w, rel_bias, mlstm_i_bias, mlstm_f_bias,
              mlstm_norm_w, gdn_conv_w, gdn_a_log, gdn_dt_bias, gdn_norm_w):
    for l in range(DEPTH):
        x = _hybrid_layer(x, norm_w[l], w_in[l], w_out[l], qk_norm_w[l], rel_bias,
                          mlstm_i_bias[l], mlstm_f_bias[l], mlstm_norm_w[l], gdn_conv_w[l],
                          gdn_a_log[l], gdn_dt_bias[l], gdn_norm_w[l])
    return x
```

```python
import numpy as np
import concourse.bass as bass
import concourse.mybir as mybir
from concourse.bass_utils import run_bass_kernel_spmd
from contextlib import ExitStack

F32 = mybir.dt.float32
BF16 = mybir.dt.bfloat16
AF = mybir.ActivationFunctionType
ALU = mybir.AluOpType
AX = mybir.AxisListType

ENG = ['tensor', 'vector', 'scalar', 'gpsimd', 'sync']
S = 8192
NPROJ = 1096


def ssl(start, n, step=1):
    return slice(start, start + (n - 1) * step + 1, step)


class Tok:
    __slots__ = ('w', 'r', 'excl')

    def __init__(self, excl=False):
        self.w = None
        self.r = {}
        self.excl = excl


def PTok():
    return Tok(True)


class Prog:
    def __init__(self, nc, n_lanes=12):
        self.nc = nc
        self.es = ExitStack()
        self.scopes = []
        self.q = {e: [] for e in ENG}
        self.cnt = {e: 0 for e in ENG}
        self.sem = {}
        for e in ENG:
            self.sem[e] = self.es.enter_context(nc.semaphore('c_' + e))
        self.lanes, self.lane_cnt, self.lane_rr = {}, {}, {}
        for e in ('sync', 'gpsimd', 'scalar'):
            self.lanes[e] = ['L%s%d' % (e, i) for i in range(n_lanes if e == 'sync' else 6)]
            for k in self.lanes[e]:
                self.sem[k] = self.es.enter_context(nc.semaphore(k))
                self.lane_cnt[k] = 0
            self.lane_rr[e] = 0
        self.waited = {}
        self.ntile = 0

    def push(self):
        self.scopes.append(ExitStack())

    def pop(self):
        self.barrier()
        self.scopes.pop().close()

    def _ctx(self):
        return self.scopes[-1] if self.scopes else self.es

    def sb(self, shape, dtype, name='t'):
        self.ntile += 1
        return self._ctx().enter_context(self.nc.sbuf_tensor('%s_%d' % (name, self.ntile), list(shape), dtype))

    def ps(self, shape, dtype, name='p'):
        self.ntile += 1
        esz = 4 if dtype == F32 else 2
        n = int(np.prod(shape[1:]))
        per_bank = 2048 // esz
        nb = (n + per_bank - 1) // per_bank
        t = self._ctx().enter_context(self.nc.psum_tensor('%s_%d' % (name, self.ntile), [128, nb * per_bank], dtype))
        v = t[0:shape[0], 0:n]
        if len(shape) == 2:
            return v
        names = ' '.join('a%d' % i for i in range(len(shape) - 1))
        kw = {'a%d' % i: shape[i + 1] for i in range(len(shape) - 1)}
        return v.rearrange('p (%s) -> p %s' % (names, names), **kw)

    def _wait(self, eng, key, val):
        if val <= 0 or self.waited.get((eng, key), 0) >= val:
            return
        self.waited[(eng, key)] = val
        getattr(self.nc, eng).wait_ge(self.sem[key], val)

    def _deps(self, eng, reads, writes):
        deps = {}

        def add(k, v):
            if deps.get(k, 0) < v:
                deps[k] = v
        for t in reads:
            if t.w is not None:
                add(*t.w)
            if t.excl:
                for k, v in t.r.items():
                    if k != eng:
                        add(k, v)
        for t in writes:
            if t.w is not None:
                add(*t.w)
            for k, v in t.r.items():
                add(k, v)
        for k, v in deps.items():
            if k == eng and eng == 'tensor':
                continue
            self._wait(eng, k, v)

    def _mark(self, ev, reads, writes):
        k, v = ev
        for t in writes:
            t.w = ev
            t.r = {}
        for t in reads:
            if t.r.get(k, 0) < v:
                t.r[k] = v

    def op(self, eng, fn, reads=(), writes=()):
        self._deps(eng, reads, writes)
        self.cnt[eng] += 1
        fn(getattr(self.nc, eng)).then_inc(self.sem[eng], 1)
        self._mark((eng, self.cnt[eng]), reads, writes)

    def dma(self, eng, out, in_, reads=(), writes=(), **kw):
        self._deps(eng, reads, writes)
        lanes = self.lanes[eng]
        k = lanes[self.lane_rr[eng] % len(lanes)]
        self.lane_rr[eng] += 1
        self._wait(eng, k, 16 * self.lane_cnt[k])
        self.lane_cnt[k] += 1
        getattr(self.nc, eng).dma_start(out=out, in_=in_, **kw).then_inc(self.sem[k], 16)
        self._mark((k, 16 * self.lane_cnt[k]), reads, writes)

    def barrier(self):
        for x in ENG:
            for k, c in self.lane_cnt.items():
                self._wait(x, k, 16 * c)
            for e in ENG:
                if e != x:
                    self._wait(x, e, self.cnt[e])

    def finish(self):
        self.barrier()
        while self.scopes:
            self.scopes.pop().close()
        self.es.close()


def mm(P, out, lhsT, rhs, start, stop, reads, writes):
    P.op('tensor', lambda e: e.matmul(out, lhsT=lhsT, rhs=rhs, start=start, stop=stop), reads=reads, writes=writes)


def tr(P, out, in_, ident, reads, writes):
    P.op('tensor', lambda e: e.transpose(out=out, in_=in_, identity=ident), reads=reads, writes=writes)


def act(P, out, in_, func, reads, writes, **kw):
    P.op('scalar', lambda e: e.activation(out=out, in_=in_, func=func, **kw), reads=reads, writes=writes)


def tt(P, eng, out, in0, in1, op, reads, writes):
    P.op(eng, lambda e: e.tensor_tensor(out=out, in0=in0, in1=in1, op=op), reads=reads, writes=writes)


def ts(P, eng, out, in0, s1, op0, reads, writes, s2=None, op1=None):
    if op1 is None:
        P.op(eng, lambda e: e.tensor_scalar(out=out, in0=in0, scalar1=s1, scalar2=None, op0=op0), reads=reads, writes=writes)
    else:
        P.op(eng, lambda e: e.tensor_scalar(out=out, in0=in0, scalar1=s1, scalar2=s2, op0=op0, op1=op1), reads=reads, writes=writes)


def stt(P, out, in0, scalar, in1, op0, op1, reads, writes):
    P.op('vector', lambda e: e.scalar_tensor_tensor(out=out, in0=in0, scalar=scalar, in1=in1, op0=op0, op1=op1),
         reads=reads, writes=writes)


def cp(P, eng, out, in_, reads, writes):
    if eng == 'scalar':
        P.op('scalar', lambda e: e.copy(out=out, in_=in_), reads=reads, writes=writes)
    else:
        P.op(eng, lambda e: e.tensor_copy(out=out, in_=in_), reads=reads, writes=writes)


class Consts:
    def __init__(self, P, cpk):
        self.t = Tok()
        self.f = P.sb([128, 6, 128], F32, 'cf')
        self.b = P.sb([128, 6, 128], BF16, 'cb')
        P.dma('sync', self.f[:], cpk, writes=[self.t])
        cp(P, 'vector', self.b[:], self.f[:], [self.t], [self.t])
        self.ones_f = P.sb([128, 128], F32, 'onesf')
        self.ones_b = P.sb([128, 128], BF16, 'onesb')
        P.op('vector', lambda e: e.memset(self.ones_f[:], 1.0), writes=[self.t])
        P.op('vector', lambda e: e.memset(self.ones_b[:], 1.0), writes=[self.t])

    def identf(self, n=128):
        return self.f[0:n, 0, 0:n]

    def identb(self, n=128):
        return self.b[0:n, 0, 0:n]


def const_pack():
    c = np.zeros((128, 6, 128), np.float32)
    c[127, 5, :] = 1.0
    c[:, 0, :] = np.eye(128)
    c[:, 1, :] = np.kron(np.eye(2), np.ones((64, 64)))
    k = np.arange(128)[:, None]
    t = np.arange(128)[None, :]
    c[:, 2, :] = (k <= t)
    c[:, 3, :] = (k < t)
    c[:, 4, :] = np.where(k <= t, 0.0, -30000.0)
    return c


PAD = 1024
import os as _os
PARTS = _os.environ.get('ATTN_PARTS', 'vsepa')
DILS = (1, 4, 16)


def emit_attn(P, C, projT, qkw_d, maskb_d, yT):
    P.push()
    qn = P.sb([128, S], BF16, 'qn')
    kn = P.sb([128, S + 2 * PAD], BF16, 'kn')
    vt = P.sb([128, S + 2 * PAD], BF16, 'vt')
    acc = [P.sb([65, S], F32, 'acc%d' % h) for h in range(2)]
    qkw = P.sb([128, 2], F32, 'qkw')
    em = P.sb([128, 3, 2, 2, 128], BF16, 'em')
    sel = P.sb([65, 64], F32, 'sel')
    t_q, t_k, t_v, t_w, t_em, t_sel = Tok(), Tok(), Tok(), Tok(), Tok(), Tok()
    t_acc = [Tok(), Tok()]
    P.dma('sync', qkw[:], qkw_d, writes=[t_w])
    P.op('vector', lambda e: e.memset(sel[:], 0.0), writes=[t_sel])
    P.op('vector', lambda e: e.memset(sel[64:65, :], 1.0), writes=[t_sel])
    P.op('gpsimd', lambda e: e.memset(kn[:, 0:PAD], 0.0), writes=[t_k])
    P.op('gpsimd', lambda e: e.memset(kn[:, PAD + S:], 0.0), writes=[t_k])
    P.op('gpsimd', lambda e: e.memset(vt[:, 0:PAD], 0.0), writes=[t_v])
    P.op('gpsimd', lambda e: e.memset(vt[:, PAD + S:], 0.0), writes=[t_v])
    P.push()
    mb = P.sb([128, 3 * 2 * 2 * 128], F32, 'mb')
    P.dma('sync', mb[:], maskb_d, writes=[t_em])
    act(P, em.rearrange("p c h a q -> p (c h a q)"), mb[:], AF.Exp, [t_em], [t_em])
    raw = [P.sb([128, 3, 512], F32, 'raw%d' % i) for i in range(2)]
    t_raw = [Tok(), Tok()]
    sq = [P.sb([128, 2, 512], BF16, 'sq%d' % i) for i in range(2)]
    t_sq = [Tok(), Tok()]
    rs = [P.sb([128, 2, 512], F32, 'rs%d' % i) for i in range(2)]
    t_rs = [Tok(), Tok()]
    pss = [P.ps([128, 2, 512], F32, 'pss%d' % i) for i in range(2)]
    t_pss = [PTok(), PTok()]
    for tg in range(16):
        i = tg % 2
        sl = slice(tg * 512, (tg + 1) * 512)
        P.dma('sync', raw[i][:], projT[0:384, sl].rearrange("(c p) t -> p c t", p=128), writes=[t_raw[i]])
        act(P, sq[i][:], raw[i][:, 0:2, :], AF.Square, [t_raw[i]], [t_sq[i]])
        for j in range(2):
            mm(P, pss[i][:, j, :], C.b[:, 1, :], sq[i][:, j, :], True, True, [t_sq[i], C.t], [t_pss[i]])
        act(P, rs[i][:], pss[i][:], AF.Ln, [t_pss[i]], [t_rs[i]], scale=1.0 / 64, bias=1e-6)
        act(P, rs[i][:, 0, :], rs[i][:, 0, :], AF.Exp, [t_rs[i]], [t_rs[i]], scale=-0.5, bias=float(np.log(0.125)))
        act(P, rs[i][:, 1, :], rs[i][:, 1, :], AF.Exp, [t_rs[i]], [t_rs[i]], scale=-0.5)
        stt(P, qn[:, sl], raw[i][:, 0, :], qkw[:, 0:1], rs[i][:, 0, :], ALU.mult, ALU.mult, [t_raw[i], t_rs[i], t_w], [t_q])
        stt(P, kn[:, PAD + tg * 512:PAD + (tg + 1) * 512], raw[i][:, 1, :], qkw[:, 1:2], rs[i][:, 1, :], ALU.mult, ALU.mult,
            [t_raw[i], t_rs[i], t_w], [t_k])
        cp(P, 'gpsimd', vt[:, PAD + tg * 512:PAD + (tg + 1) * 512], raw[i][:, 2, :], [t_raw[i]], [t_v])
    P.pop()
    import os as _os
    if _os.environ.get('ATTN_STAGE') == '1':
        P.pop()
        return
    P.push()
    NS = 2
    vx = [[P.sb([128, 2, 66], BF16, 'vx') for _ in range(5)] for _ in range(NS)]
    t_vx = [[Tok() for _ in range(5)] for _ in range(NS)]
    vxe = [P.sb([128, 2, 66], BF16, 'vxe') for _ in range(2)]
    t_vxe = [Tok(), Tok()]
    for s_ in range(NS):
        for b_ in range(5):
            P.op('gpsimd', lambda e, s_=s_, b_=b_: e.memset(vx[s_][b_][:, :, 64:65], 1.0), writes=[t_vx[s_][b_]])
    P.op('gpsimd', lambda e: e.memset(vxe[0][0:64, :, 64:65], 0.0), writes=[t_vxe[0]])
    P.op('gpsimd', lambda e: e.memset(vxe[0][64:128, :, 64:65], 1.0), writes=[t_vxe[0]])
    P.op('gpsimd', lambda e: e.memset(vxe[1][0:64, :, 64:65], 1.0), writes=[t_vxe[1]])
    P.op('gpsimd', lambda e: e.memset(vxe[1][64:128, :, 64:65], 0.0), writes=[t_vxe[1]])
    pv = [P.ps([128, 128], BF16, 'pv') for _ in range(2)]
    t_pv = [PTok(), PTok()]
    pS = [[P.ps([128, 2, 128], F32, 'pS') for _ in range(2)] for _ in range(2)]
    t_pS = [[PTok(), PTok()], [PTok(), PTok()]]
    pnd = [[P.ps([65, 512], F32, 'pnd') for _ in range(2)] for _ in range(1)]
    t_pnd = [[PTok(), PTok()]]
    pe = [P.sb([128, 2, 2, 128], BF16, 'pe') for _ in range(2)]
    t_pe = [Tok(), Tok()]
    pt = [P.sb([128, 2, 2, 128], BF16, 'pt') for _ in range(2)]
    t_pt = [Tok(), Tok()]
    nvx = 0
    nq = 0
    ngrp = 0
    for ci, d in enumerate(DILS):
        if _os.environ.get('ATTN_CFGS') and str(ci) not in _os.environ.get('ATTN_CFGS'):
            continue
        n_d = S // d
        ntile = n_d // 128
        for r in range(d):
            for G in range(ntile // 4):
                set_ = ngrp % NS
                vb, tvb = [], []
                for b_ in range(5):
                    j = 4 * G + b_
                    if j == 0:
                        buf, tk = vxe[0], t_vxe[0]
                    elif j == ntile:
                        buf, tk = vxe[1], t_vxe[1]
                    else:
                        buf, tk = vx[set_][b_], t_vx[set_][b_]
                    t0 = PAD + r + d * (128 * j - 64)
                    pi = nvx % 2
                    nvx += 1
                    if 'v' in PARTS:
                        tr(P, pv[pi][:], vt[:, ssl(t0, 128, d)], C.identb(), [t_v, C.t], [t_pv[pi]])
                        cp(P, 'gpsimd' if False else 'vector', buf[:, :, 0:64], pv[pi].rearrange("p (h e) -> p h e", h=2), [t_pv[pi]], [tk])
                    vb.append(buf)
                    tvb.append(tk)
                gi = 0
                for i in range(4):
                    m = 4 * G + i
                    si = nq % 2
                    nq += 1
                    q0 = r + d * 128 * m
                    for h in ([int(_os.environ['ATTN_H'])] if 'ATTN_H' in _os.environ else range(2 if 's' in PARTS else 0)):
                        for a_ in range(2):
                            j = m + a_
                            k0 = PAD + r + d * (128 * j - 64)
                            mm(P, pS[si][h][:, a_, :], kn[64 * h:64 * h + 64, ssl(k0, 128, d)],
                               qn[64 * h:64 * h + 64, ssl(q0, 128, d)], True, True, [t_k, t_q], [t_pS[si][h]])
                    if 'e' in PARTS:
                        for h in range(2):
                            act(P, pe[si][:, h], pS[si][h][:], AF.Exp, [t_pS[si][h]], [t_pe[si]])
                        tt(P, 'vector', pt[si][:], pe[si][:], em[:, ci], ALU.mult, [t_pe[si], t_em], [t_pt[si]])
                    for h in range(2 if 'p' in PARTS else 0):
                        for a_ in range(2):
                            mm(P, pnd[gi][h][:, i * 128:(i + 1) * 128], vb[i + a_][:, h, 0:65], pt[si][:, h, a_, :],
                               a_ == 0, a_ == 1, [tvb[i + a_], t_pt[si]], [t_pnd[gi][h]])
                u0 = r + d * 512 * G
                for h in range(2 if 'a' in PARTS else 0):
                    dst = acc[h][:, ssl(u0, 512, d)]
                    if ci == 0:
                        cp(P, 'gpsimd' if False else 'scalar', dst, pnd[gi][h][:], [t_pnd[gi][h]], [t_acc[h]])
                    else:
                        tt(P, 'vector', dst, dst, pnd[gi][h][:], ALU.add, [t_pnd[gi][h], t_acc[h]], [t_acc[h]])
                ngrp += 1
    P.pop()
    if _os.environ.get('ATTN_STAGE') == '2':
        P.pop()
        return
    P.push()
    pden = [P.ps([64, 512], F32, 'pden') for _ in range(2)]
    t_pden = [PTok(), PTok()]
    zr = [P.sb([64, 512], F32, 'zr') for _ in range(2)]
    t_zr = [Tok(), Tok()]
    rd = [P.sb([64, 512], F32, 'rd') for _ in range(2)]
    t_rd = [Tok(), Tok()]
    yo = [P.sb([64, 512], F32, 'yo') for _ in range(2)]
    t_yo = [Tok(), Tok()]
    n = 0
    for tg in range(16):
        sl = slice(tg * 512, (tg + 1) * 512)
        for h in range(2):
            i = n % 2
            n += 1
            P.dma('sync', zr[i][:], projT[384 + 64 * h:448 + 64 * h, sl], writes=[t_zr[i]])
            mm(P, pden[i][:], sel[:], acc[h][:, sl], True, True, [t_sel, t_acc[h]], [t_pden[i]])
            P.op('vector', lambda e, i=i: e.reciprocal(out=rd[i][:], in_=pden[i][:]), reads=[t_pden[i]], writes=[t_rd[i]])
            act(P, zr[i][:], zr[i][:], AF.Silu, [t_zr[i]], [t_zr[i]])
            tt(P, 'vector', rd[i][:], rd[i][:], acc[h][0:64, sl], ALU.mult, [t_rd[i], t_acc[h]], [t_rd[i]])
            tt(P, 'gpsimd', yo[i][:], rd[i][:], zr[i][:], ALU.mult, [t_rd[i], t_zr[i]], [t_yo[i]])
            P.dma('gpsimd', yT[64 * h:64 * h + 64, sl], yo[i][:], reads=[t_yo[i]])
    P.pop()
    P.pop()


def attn_mask_tables(rel_bias, g):
    half = 16

    def bucket(rel):
        max_exact = half // 2
        n = np.abs(rel)
        large = max_exact + (np.log(np.maximum(n, 1) / max_exact) / np.log(1024 / max_exact) * (half - max_exact)).astype(np.int32)
        large = np.minimum(large, half - 1)
        return (rel > 0).astype(np.int32) * half + np.where(n < max_exact, n, large)
    s = np.arange(128)[:, None]
    ql = np.arange(128)[None, :]
    out = np.full((128, 3, 2, 2, 128), -30000.0, np.float32)
    for ci, d in enumerate(DILS):
        for a_ in range(2):
            du = (s - 64 + 128 * a_) - ql
            valid = np.abs(du) <= 64
            bk = bucket(du * d)
            for h in range(2):
                vals = rel_bias[bk, 2 * g + h]
                out[:, ci, h, a_, :] = np.where(valid, vals, -30000.0)
    return out.reshape(128, -1)


def rsl(c, L=128):
    start = S - 1 - L * c
    stop = start - L
    return slice(start, stop if stop >= 0 else None, -1)


def cview(ap2d, dr, c, L=128):
    return ap2d[:, L * c:L * (c + 1)] if dr == 0 else ap2d[:, rsl(c, L)]


LN8 = float(np.log(8.0))


def emit_mlstm(P, C, projT, bpar_d, yT):
    P.push()
    NCH = S // 128
    qT = P.sb([64, S], BF16, 'bq')
    kvT = P.sb([128, S], BF16, 'bkv')
    brow = P.sb([33, S], F32, 'brow')
    hT = P.sb([64, S], F32, 'hT')
    bpar = P.sb([128, 5], F32, 'bpar')
    nfb = P.sb([128, 2], F32, 'nfb')
    Gd = P.sb([128, NCH, 2], F32, 'Gd')
    ib0 = P.sb([128, NCH], F32, 'ib0')
    ibs = P.sb([128, NCH], F32, 'ibs')
    eb = P.sb([128, NCH], F32, 'eb')
    wq = P.sb([128, NCH], F32, 'wq')
    ebL = P.sb([64, NCH], F32, 'ebL')
    t_q, t_kv, t_brow, t_h, t_bp, t_g = Tok(), Tok(), Tok(), Tok(), Tok(), Tok()
    P.dma('sync', bpar[:], bpar_d, writes=[t_bp])
    ts(P, 'vector', nfb[:], bpar[:, 0:2], -1.0, ALU.mult, [t_bp], [t_bp])
    P.op('gpsimd', lambda e: e.memset(brow[:], 0.0), writes=[t_brow])
    for dr in range(2):
        P.push()
        rows = P.sb([33, S], F32, 'rows')
        rmask = P.sb([1, S], BF16, 'rmask')
        t_rows, t_rm = Tok(), Tok()
        P.op('vector', lambda e: e.memset(rmask[:], 1.0), writes=[t_rm])
        P.op('vector', lambda e: e.memset(rmask[:, 0:S:128], 0.0), writes=[t_rm])
        P.dma('sync', rows[0:1, :], projT[1090 + dr:1091 + dr, :], writes=[t_rows])
        P.dma('sync', rows[32:33, :], projT[1088 + dr:1089 + dr, :], writes=[t_rows])
        ld = [P.sb([128, 2, 1024], F32, 'ld') for _ in range(2)]
        t_ld = [Tok(), Tok()]
        for tg in range(8):
            i = tg % 2
            sl = slice(tg * 1024, (tg + 1) * 1024)
            P.dma('sync', ld[i][0:64, 0, :], projT[512:576, sl], writes=[t_ld[i]])
            P.dma('sync', ld[i][:, 1, :], projT[576:704, sl], writes=[t_ld[i]])
            if dr == 0:
                cp(P, 'vector', qT[:, sl], ld[i][0:64, 0, :], [t_ld[i]], [t_q])
                cp(P, 'gpsimd', kvT[:, sl], ld[i][:, 1, :], [t_ld[i]], [t_kv])
            else:
                dl = slice(S - (tg + 1) * 1024, S - tg * 1024)
                cp(P, 'vector', qT[:, dl], ld[i][0:64, 0, ::-1], [t_ld[i]], [t_q])
                cp(P, 'gpsimd', kvT[:, dl], ld[i][:, 1, ::-1], [t_ld[i]], [t_kv])
        act(P, rows[0:1, :], rows[0:1, :], AF.Exp, [t_rows, t_bp], [t_rows], scale=-1.0, bias=nfb[0:1, dr:dr + 1])
        act(P, rows[0:1, :], rows[0:1, :], AF.Ln, [t_rows], [t_rows], bias=1.0)
        ts(P, 'vector', rows[0:1, :], rows[0:1, :], -1.0, ALU.mult, [t_rows], [t_rows])
        fsrc = rows[0:1, :] if dr == 0 else rows[0:1, ::-1]
        isrc = rows[32:33, :] if dr == 0 else rows[32:33, ::-1]
        P.op('vector', lambda e, fsrc=fsrc: e.tensor_tensor_scan(out=brow[0:1, :], data0=rmask[0:1, :], data1=fsrc, initial=0.0,
                                                                 op0=ALU.mult, op1=ALU.add), reads=[t_rows, t_rm], writes=[t_brow])
        cp(P, 'vector', brow[32:33, :], isrc, [t_rows], [t_brow])
        pG = [P.ps([128, 15, 33], F32, 'pG') for _ in range(2)]
        t_pG = [PTok(), PTok()]
        pbl = P.ps([128, NCH], F32, 'pbl')
        t_pbl = PTok()
        nb = 0
        for c0 in range(0, NCH, 15):
            n = min(15, NCH - c0)
            i = nb % 2
            nb += 1
            for k in range(n):
                tr(P, pG[i][:, k, :], brow[:, 128 * (c0 + k):128 * (c0 + k + 1)], C.identf(33), [t_brow, C.t], [t_pG[i]])
            cp(P, 'vector', Gd[:, c0:c0 + n, 0], pG[i][:, 0:n, 0], [t_pG[i]], [t_g])
            cp(P, 'scalar', Gd[:, c0:c0 + n, 1], pG[i][:, 0:n, 32], [t_pG[i]], [t_g])
        b_ = Gd[:, :, 0]
        i_ = Gd[:, :, 1]
        tt(P, 'vector', ib0[:], i_, b_, ALU.subtract, [t_g], [t_g])
        ts(P, 'vector', ib0[:], ib0[:], bpar[:, 2 + dr:3 + dr], ALU.add, [t_g, t_bp], [t_g])
        ts(P, 'vector', ibs[:], ib0[:], -LN8, ALU.add, [t_g], [t_g])
        act(P, eb[:], b_, AF.Exp, [t_g], [t_g], bias=-LN8)
        mm(P, pbl, C.ones_f[0:1, :], brow[0:1, 127:S:128], True, True, [t_brow, C.t], [t_pbl])
        act(P, ebL[:], pbl[0:64, :], AF.Exp, [t_pbl], [t_g])
        tt(P, 'vector', wq[:], pbl, ib0[:], ALU.add, [t_pbl, t_g], [t_g])
        act(P, wq[:], wq[:], AF.Exp, [t_g], [t_g])
        P.pop()
        P.push()
        pKV = P.ps([128, 128], BF16, 'pKV'); t_pKV = PTok()
        pS = [P.ps([128, 128], F32, 'pS') for _ in range(2)]; t_pS = [PTok(), PTok()]
        pB = [P.ps([128, 128], F32, 'pB') for _ in range(2)]; t_pB = [PTok(), PTok()]
        pI = P.ps([128, 256], F32, 'pI'); t_pI = PTok()
        pC = P.ps([64, 65], F32, 'pC'); t_pC = PTok()
        pO = P.ps([64, 128], F32, 'pO'); t_pO = PTok()
        kvt = [P.sb([128, 130], BF16, 'kvt') for _ in range(2)]; t_kvt = [Tok(), Tok()]
        vw = [P.sb([128, 66], BF16, 'vw') for _ in range(2)]; t_vw = [Tok(), Tok()]
        Dm = [P.sb([128, 128], F32, 'Dm') for _ in range(2)]; t_D = [Tok(), Tok()]
        PT = [P.sb([128, 128], BF16, 'PT') for _ in range(2)]; t_PT = [Tok(), Tok()]
        tmp = P.sb([128, 65], F32, 'tmp'); t_tmp = Tok()
        nd = P.sb([128, 65], F32, 'nd'); t_nd = Tok()
        dn = P.sb([128, 2], F32, 'dn'); t_dn = Tok()
        hd = [P.sb([128, 64], F32, 'hd') for _ in range(2)]; t_hd = [Tok(), Tok()]
        Cf = P.sb([64, 65], F32, 'Cf'); Cb = P.sb([64, 66], BF16, 'Cb'); t_C = Tok(); t_Cb = Tok()
        for i in range(2):
            P.op('vector', lambda e, i=i: e.memset(kvt[i][:, 128:129], 1.0), writes=[t_kvt[i]])
        P.op('vector', lambda e: e.memset(Cf[:], 0.0), writes=[t_C])
        P.op('vector', lambda e: e.memset(Cb[:], 0.0), writes=[t_Cb])
        for c in range(NCH):
            i = c % 2
            csl = slice(128 * c, 128 * (c + 1))
            qc = qT[:, csl]
            kc = kvT[0:64, csl]
            tr(P, pKV, kvT[:, csl], C.identb(), [t_kv, C.t], [t_pKV])
            cp(P, 'scalar', kvt[i][:, 0:128], pKV, [t_pKV], [t_kvt[i]])
            ts(P, 'gpsimd', vw[i][:, 0:65], kvt[i][:, 64:129], wq[:, c:c + 1], ALU.mult, [t_kvt[i], t_g], [t_vw[i]])
            mm(P, pS[i], kc, qc, True, True, [t_kv, t_q], [t_pS[i]])
            mm(P, pB[i], C.ones_f[0:1, :], brow[0:1, csl], True, False, [t_brow, C.t], [t_pB[i]])
            mm(P, pB[i], C.identf(), C.f[:, 4, :], False, True, [C.t], [t_pB[i]])
            act(P, Dm[i][:], pB[i], AF.Exp, [t_pB[i], t_g], [t_D[i]], bias=ibs[:, c:c + 1])
            tt(P, 'vector', PT[i][:], pS[i], Dm[i][:], ALU.mult, [t_pS[i], t_D[i]], [t_PT[i]])
            mm(P, pI[:, 0:65], PT[i][:], kvt[i][:, 64:129], True, True, [t_PT[i], t_kvt[i]], [t_pI])
            mm(P, pI[:, 128:193], qc, Cb[:, 0:65], True, True, [t_q, t_Cb], [t_pI])
            ts(P, 'vector', tmp[:], pI[:, 128:193], eb[:, c:c + 1], ALU.mult, [t_pI, t_g], [t_tmp])
            tt(P, 'vector', nd[:], tmp[:], pI[:, 0:65], ALU.add, [t_tmp, t_pI], [t_nd])
            stt(P, dn[:, 0:1], nd[:, 64:65], -1.0, nd[:, 64:65], ALU.mult, ALU.max, [t_nd], [t_dn])
            ts(P, 'vector', dn[:, 0:1], dn[:, 0:1], 1.0, ALU.max, [t_dn], [t_dn])
            P.op('vector', lambda e: e.reciprocal(out=dn[:, 1:2], in_=dn[:, 0:1]), reads=[t_dn], writes=[t_dn])
            ts(P, 'gpsimd', hd[i][:], nd[:, 0:64], dn[:, 1:2], ALU.mult, [t_nd, t_dn], [t_hd[i]])
            tr(P, pO, hd[i][:], C.identf(), [t_hd[i], C.t], [t_pO])
            if dr == 0:
                cp(P, 'scalar', hT[:, csl], pO, [t_pO], [t_h])
            else:
                hv = hT[:, rsl(c)]
                tt(P, 'vector', hv, hv, pO, ALU.add, [t_pO, t_h], [t_h])
            mm(P, pC, kvt[i][:, 0:64], vw[i][:, 0:65], True, True, [t_kvt[i], t_vw[i]], [t_pC])
            stt(P, Cf[:], Cf[:], ebL[:, c:c + 1], pC, ALU.mult, ALU.add, [t_C, t_pC, t_g], [t_C])
            cp(P, 'scalar', Cb[:, 0:65], Cf[:], [t_C], [t_Cb])
        P.pop()
    emit_norm_gate(P, C, hT, t_h, projT, 768, 704, bpar[0:64, 4:5], t_bp, yT, 128)
    P.pop()


def emit_norm_gate(P, C, hT, t_h, projT, zrow, orow, w_ap, t_w, yT, yrow):
    P.push()
    oz = [P.sb([64, 2, 512], F32, 'oz') for _ in range(2)]; t_oz = [Tok(), Tok()]
    hb = [P.sb([64, 512], F32, 'hb') for _ in range(2)]; t_hb = [Tok(), Tok()]
    sqb = [P.sb([64, 512], BF16, 'sqb') for _ in range(2)]; t_sqb = [Tok(), Tok()]
    pss = [P.ps([64, 512], F32, 'pssb') for _ in range(2)]; t_pss = [PTok(), PTok()]
    rs = [P.sb([64, 512], F32, 'rsb') for _ in range(2)]; t_rs = [Tok(), Tok()]
    yo = [P.sb([64, 512], F32, 'yob') for _ in range(2)]; t_yo = [Tok(), Tok()]
    for tg in range(16):
        i = tg % 2
        sl = slice(tg * 512, (tg + 1) * 512)
        P.dma('sync', oz[i][:, 1, :], projT[zrow:zrow + 64, sl], writes=[t_oz[i]])
        act(P, oz[i][:, 1, :], oz[i][:, 1, :], AF.Silu, [t_oz[i]], [t_oz[i]])
        if orow is not None:
            P.dma('sync', oz[i][:, 0, :], projT[orow:orow + 64, sl], writes=[t_oz[i]])
            act(P, oz[i][:, 0, :], oz[i][:, 0, :], AF.Sigmoid, [t_oz[i]], [t_oz[i]])
            tt(P, 'vector', hb[i][:], hT[:, sl], oz[i][:, 0, :], ALU.mult, [t_h, t_oz[i]], [t_hb[i]])
            hsrc, t_hs = hb[i][:], t_hb[i]
        else:
            hsrc, t_hs = hT[:, sl], t_h
        tt(P, 'gpsimd', sqb[i][:], hsrc, hsrc, ALU.mult, [t_hs], [t_sqb[i]])
        mm(P, pss[i], C.ones_b[0:64, 0:64], sqb[i][:], True, True, [t_sqb[i], C.t], [t_pss[i]])
        act(P, rs[i][:], pss[i], AF.Ln, [t_pss[i]], [t_rs[i]], scale=1.0 / 64, bias=1e-6)
        act(P, rs[i][:], rs[i][:], AF.Exp, [t_rs[i]], [t_rs[i]], scale=-0.5)
        stt(P, yo[i][:], hsrc, w_ap, rs[i][:], ALU.mult, ALU.mult, [t_hs, t_rs[i], t_w], [t_yo[i]])
        tt(P, 'gpsimd', yo[i][:], yo[i][:], oz[i][:, 1, :], ALU.mult, [t_yo[i], t_oz[i]], [t_yo[i]])
        P.dma('gpsimd', yT[yrow:yrow + 64, sl], yo[i][:], reads=[t_yo[i]])
    P.pop()


def emit_gdn(P, C, projT, cpar_d, yT):
    P.push()
    NCH = S // 128
    cpar = P.sb([128, 16], F32, 'cpar'); t_cp = Tok()
    nea = P.sb([128, 2], F32, 'nea')
    qbias = P.sb([128, 1], F32, 'qbias')
    kq2n = P.sb([64, 2, S], BF16, 'kq2n')
    vkn = P.sb([128, S], BF16, 'vkn')
    oT = P.sb([64, S], F32, 'oT')
    t_kq, t_vk, t_o = Tok(), Tok(), Tok()
    P.dma('sync', cpar[:], cpar_d, writes=[t_cp])
    act(P, nea[:], cpar[:, 10:12], AF.Exp, [t_cp], [t_cp])
    ts(P, 'vector', nea[:], nea[:], -1.0, ALU.mult, [t_cp], [t_cp])
    P.op('vector', lambda e: e.memset(qbias[0:64, :], -LN8), writes=[t_cp])
    P.op('vector', lambda e: e.memset(qbias[64:128, :], 0.0), writes=[t_cp])
    P.push()
    NB = 2048
    for blk in range(S // NB):
        P.push()
        t0 = blk * NB
        u = P.sb([128, 2, NB + 4], F32, 'u'); t_u = Tok()
        cv = P.sb([128, 2, NB], F32, 'cv'); t_cv = Tok()
        lo = max(t0 - 2, 0)
        hi = min(t0 + NB + 2, S)
        if lo > t0 - 2:
            P.op('gpsimd', lambda e: e.memset(u[:, :, 0:2], 0.0), writes=[t_u])
        if hi < t0 + NB + 2:
            P.op('gpsimd', lambda e: e.memset(u[:, :, NB + 2:NB + 4], 0.0), writes=[t_u])
        o0 = lo - (t0 - 2)
        P.dma('sync', u[:, 0, o0:o0 + hi - lo], projT[896:1024, lo:hi], writes=[t_u])
        P.dma('sync', u[0:64, 1, o0:o0 + hi - lo], projT[1024:1088, lo:hi], writes=[t_u])
        for j, (np_, c0) in enumerate(((128, 0), (64, 5))):
            ts(P, 'vector', cv[0:np_, j, :], u[0:np_, j, 0:NB], cpar[0:np_, c0:c0 + 1], ALU.mult, [t_u, t_cp], [t_cv])
            for w in range(1, 5):
                stt(P, cv[0:np_, j, :], u[0:np_, j, w:w + NB], cpar[0:np_, c0 + w:c0 + w + 1], cv[0:np_, j, :], ALU.mult, ALU.add,
                    [t_u, t_cp, t_cv], [t_cv])
        act(P, cv[:, 0, :], cv[:, 0, :], AF.Silu, [t_cv], [t_cv])
        act(P, vkn[0:64, t0:t0 + NB], cv[0:64, 1, :], AF.Silu, [t_cv], [t_vk])
        sq = P.sb([128, NB], BF16, 'sq'); t_sq = Tok()
        tt(P, 'gpsimd', sq[:], cv[:, 0, :], cv[:, 0, :], ALU.mult, [t_cv], [t_sq])
        pss = [P.ps([128, 512], F32, 'pssc') for _ in range(2)]; t_pss = [PTok(), PTok()]
        rs = [P.sb([128, 512], F32, 'rsc') for _ in range(2)]; t_rs = [Tok(), Tok()]
        for k in range(NB // 512):
            i = k % 2
            sl = slice(k * 512, (k + 1) * 512)
            mm(P, pss[i], C.b[:, 1, :], sq[:, sl], True, True, [t_sq, C.t], [t_pss[i]])
            act(P, rs[i][:], pss[i], AF.Ln, [t_pss[i]], [t_rs[i]], bias=1e-6)
            act(P, rs[i][:], rs[i][:], AF.Exp, [t_rs[i], t_cp], [t_rs[i]], scale=-0.5, bias=qbias[:, 0:1])
            tt(P, 'vector', kq2n[:, 1, t0 + k * 512:t0 + (k + 1) * 512], cv[0:64, 0, sl], rs[i][0:64, :], ALU.mult,
               [t_cv, t_rs[i]], [t_kq])
            tt(P, 'vector', vkn[64:128, t0 + k * 512:t0 + (k + 1) * 512], cv[64:128, 0, sl], rs[i][64:128, :], ALU.mult,
               [t_cv, t_rs[i]], [t_vk])
        P.pop()
    P.dma('sync', kq2n[:, 0, :], vkn[64:128, :], reads=[t_vk], writes=[t_kq])
    P.pop()
    GST = int(_os.environ.get('GDN_STAGE', '9'))
    GPF = int(_os.environ.get('GDN_PF', '9'))
    for dr in range(2 if GST > 1 else 0):
        P.push()
        if dr == 0:
            kq2, vk = kq2n, vkn
        else:
            kq2 = P.sb([64, 2, S], BF16, 'kq2r')
            vk = P.sb([128, S], BF16, 'vkr')
            for k in range(4):
                a, b = k * 2048, (k + 1) * 2048
                cp(P, 'vector', kq2[:, :, S - b:S - a], kq2n[:, :, a:b][:, :, ::-1], [t_kq], [t_kq])
                cp(P, 'gpsimd', vk[:, S - b:S - a], vkn[:, a:b][:, ::-1], [t_vk], [t_vk])
        Gd = P.sb([128, NCH, 2], F32, 'Gdc'); t_g = Tok()
        eg = P.sb([128, NCH], F32, 'eg')
        ekd = P.sb([128, NCH], F32, 'ekd')
        nbeta = P.sb([128, NCH], F32, 'nbeta')
        egl = P.sb([64, NCH], F32, 'egl')
        P.push()
        SEG = 2048
        rmask = P.sb([1, SEG], BF16, 'rmaskc'); t_rm = Tok()
        P.op('vector', lambda e: e.memset(rmask[:], 1.0), writes=[t_rm])
        P.op('vector', lambda e: e.memset(rmask[:, 0:SEG:128], 0.0), writes=[t_rm])
        rows = [P.sb([33, SEG], F32, 'rowsc') for _ in range(2)]; t_rows = [Tok(), Tok()]
        grow = [P.sb([33, SEG], F32, 'grow') for _ in range(2)]; t_gr = [Tok(), Tok()]
        pG = [P.ps([128, 8, 33], F32, 'pGc') for _ in range(2)]; t_pG = [PTok(), PTok()]
        pbl = P.ps([128, NCH], F32, 'pblc'); t_pbl = PTok()
        for i in range(2):
            P.op('gpsimd', lambda e, i=i: e.memset(grow[i][:], 0.0), writes=[t_gr[i]])
            P.op('gpsimd', lambda e, i=i: e.memset(rows[i][:], 0.0), writes=[t_rows[i]])
        nb = 0
        for sg in range(S // SEG):
            i = sg % 2
            n0 = sg * SEG if dr == 0 else S - (sg + 1) * SEG
            rw, t_rw = rows[i], t_rows[i]
            P.dma('sync', rw[0:1, :], projT[1092 + dr:1093 + dr, n0:n0 + SEG], writes=[t_rw])
            P.dma('sync', rw[32:33, :], projT[1094 + dr:1095 + dr, n0:n0 + SEG], writes=[t_rw])
            act(P, rw[0:1, :], rw[0:1, :], AF.Exp, [t_rw, t_cp], [t_rw], bias=cpar[0:1, 12 + dr:13 + dr])
            act(P, rw[0:1, :], rw[0:1, :], AF.Ln, [t_rw], [t_rw], bias=1.0)
            ts(P, 'vector', rw[0:1, :], rw[0:1, :], nea[0:1, dr:dr + 1], ALU.mult, [t_rw, t_cp], [t_rw])
            act(P, rw[32:33, :], rw[32:33, :], AF.Exp, [t_rw], [t_rw], scale=-1.0)
            ts(P, 'vector', rw[32:33, :], rw[32:33, :], 1.0, ALU.add, [t_rw], [t_rw])
            P.op('vector', lambda e, rw=rw: e.reciprocal(out=rw[32:33, :], in_=rw[32:33, :]), reads=[t_rw], writes=[t_rw])
            gsrc = rw[0:1, :] if dr == 0 else rw[0:1, ::-1]
            bsrc = rw[32:33, :] if dr == 0 else rw[32:33, ::-1]
            gw, t_gw = grow[i], t_gr[i]
            P.op('vector', lambda e, gw=gw, gsrc=gsrc: e.tensor_tensor_scan(out=gw[0:1, :], data0=rmask[0:1, :], data1=gsrc, initial=0.0,
                                                                            op0=ALU.mult, op1=ALU.add), reads=[t_rw, t_rm], writes=[t_gw])
            cp(P, 'vector', gw[32:33, :], bsrc, [t_rw], [t_gw])
            for hb_ in range(2):
                j = nb % 2
                nb += 1
                c0 = sg * 16 + hb_ * 8
                for k in range(8):
                    tr(P, pG[j][:, k, :], gw[:, 128 * (hb_ * 8 + k):128 * (hb_ * 8 + k + 1)], C.identf(33), [t_gw, C.t], [t_pG[j]])
                cp(P, 'vector', Gd[:, c0:c0 + 8, 0], pG[j][:, :, 0], [t_pG[j]], [t_g])
                cp(P, 'scalar', Gd[:, c0:c0 + 8, 1], pG[j][:, :, 32], [t_pG[j]], [t_g])
        act(P, eg[:], Gd[:, :, 0], AF.Exp, [t_g], [t_g])
        ts(P, 'vector', nbeta[:], Gd[:, :, 1], -1.0, ALU.mult, [t_g], [t_g])
        mm(P, pbl, C.f[:, 5, :], Gd[:, :, 0], True, True, [t_g, C.t], [t_pbl])
        act(P, egl[:], pbl[0:64, :], AF.Exp, [t_pbl], [t_g])
        tt(P, 'vector', ekd[:], pbl, Gd[:, :, 0], ALU.subtract, [t_pbl, t_g], [t_g])
        act(P, ekd[:], ekd[:], AF.Exp, [t_g], [t_g])
        P.pop()
        P.push()
        bankA = P.ps([128, 512], F32, 'bankA'); t_pGb = PTok()
        pGb = bankA[:, 0:128]
        pK = P.ps([128, 2, 128], F32, 'pK'); t_pK = PTok()
        pN = [P.ps([128, 128], F32, 'pN') for _ in range(3)]; t_pN = [PTok(), PTok(), PTok()]
        pTU = P.ps([128, 128], F32, 'pTU'); t_pTU = PTok()
        pTb = P.ps([128, 128], BF16, 'pTb'); t_pTb = t_pTU
        pGH = P.ps([64, 128], F32, 'pGH'); t_pGH = PTok()
        pOS = bankA[0:64, 128:320]; t_pOS = t_pGb
        dm = P.sb([128, 128], F32, 'dm'); t_dm = Tok()
        dg = P.sb([128, 128], F32, 'dg'); t_dg = Tok()
        decS = P.sb([128, 128], F32, 'decS'); decI = P.sb([128, 128], F32, 'decI'); t_dec = Tok()
        eg64 = P.sb([64, 128], F32, 'eg64'); t_eg64 = Tok()
        Ya = [P.sb([128, 128], F32, 'Ya') for _ in range(2)]; t_Ya = [Tok(), Tok()]
        Yt = [P.sb([128, 128], F32, 'Yt') for _ in range(2)]; t_Yt = [Tok(), Tok()]
        Rm = [P.sb([128, 128], F32, 'Rm') for _ in range(2)]; t_R = [Tok(), Tok()]
        AT = [P.sb([128, 128], F32, 'AT') for _ in range(2)]; t_AT = [Tok(), Tok()]
        Bm = P.sb([128, 128], F32, 'Bm'); t_B = Tok()
        kdec = P.sb([128, 64], F32, 'kdec'); t_kd = Tok()
        UW = [P.sb([128, 128], F32, 'UW') for _ in range(2)]; t_UW = [Tok(), Tok()]
        Gt = P.sb([64, 64], F32, 'Gt'); Hm = P.sb([64, 64], F32, 'Hm'); t_GH = Tok()
        qg = P.sb([64, 128], F32, 'qg'); t_qg = Tok()
        Qe = P.sb([64, 128], F32, 'Qe'); t_Qe = Tok()
        Sst = [P.sb([64, 64], F32, 'Sst') for _ in range(2)]; t_S = [Tok(), Tok()]
        P.op('vector', lambda e: e.memset(Sst[0][:], 0.0), writes=[t_S[0]])
        for c in range(NCH if GST > 2 else 0):
            i = c % 2
            csl = slice(128 * c, 128 * (c + 1))
            ts(P, 'gpsimd', dg[:], C.f[:, 0, :], Gd[:, c, 0:1], ALU.mult, [C.t, t_g], [t_dg])
            mm(P, pGb, C.ones_f[:], dg[:], True, True, [t_dg, C.t], [t_pGb])
            ts(P, 'vector', dm[:], pGb, Gd[:, c, 0:1], ALU.subtract, [t_pGb, t_g], [t_dm], s2=0.0, op1=ALU.min)
            act(P, eg64[:], pGb[0:64, :], AF.Exp, [t_pGb], [t_eg64])
            act(P, dm[:], dm[:], AF.Exp, [t_dm], [t_dm])
            tt(P, 'gpsimd', decS[:], dm[:], C.f[:, 3, :], ALU.mult, [t_dm, C.t], [t_dec])
            tt(P, 'gpsimd', decI[:], dm[:], C.f[:, 2, :], ALU.mult, [t_dm, C.t], [t_dec])
            if GPF < 2:
                continue
            mm(P, pK, kq2[:, 0, csl], kq2[:, :, csl], True, True, [t_kq], [t_pK])
            stt(P, Ya[0][:], pK[:, 0, :], nbeta[:, c:c + 1], decS[:], ALU.mult, ALU.mult, [t_pK, t_dec, t_g], [t_Ya[0]])
            tt(P, 'vector', AT[i][:], pK[:, 1, :], decI[:], ALU.mult, [t_pK, t_dec], [t_AT[i]])
            if GPF < 3:
                continue
            tr(P, pN[0], Ya[0][:], C.identf(), [t_Ya[0], C.t], [t_pN[0]])
            cp(P, 'scalar', Yt[0][:], pN[0], [t_pN[0]], [t_Yt[0]])
            tt(P, 'gpsimd', Rm[0][:], Ya[0][:], C.f[:, 0, :], ALU.add, [t_Ya[0], C.t], [t_R[0]])
            ya, ri = 0, 0
            for lvl in range(6):
                last = lvl == 5
                yb = 1 - ya
                if not last:
                    mm(P, pN[0], Yt[ya][:], Ya[ya][:], True, True, [t_Yt[ya], t_Ya[ya]], [t_pN[0]])
                mm(P, pN[1], Ya[ya][:], Yt[ya][:], True, True, [t_Yt[ya], t_Ya[ya]], [t_pN[1]])
                if not last:
                    cp(P, 'scalar', Ya[yb][:], pN[0], [t_pN[0]], [t_Ya[yb]])
                cp(P, 'vector', Yt[yb][:], pN[1], [t_pN[1]], [t_Yt[yb]])
                mm(P, pN[2], Yt[yb][:], Rm[ri][:], True, True, [t_Yt[yb], t_R[ri]], [t_pN[2]])
                tt(P, 'vector', Rm[1 - ri][:], Rm[ri][:], pN[2], ALU.add, [t_R[ri], t_pN[2]], [t_R[1 - ri]])
                ya, ri = yb, 1 - ri
            RT, t_RT = Rm[ri], t_R[ri]
            if GPF < 4:
                continue
            tr(P, pTb, vk[:, csl], C.identb(), [t_vk, C.t], [t_pTb])
            cp(P, 'scalar', Bm[:, 0:64], pTb[:, 0:64], [t_pTb], [t_B])
            ts(P, 'vector', Bm[:, 64:128], pTb[:, 64:128], eg[:, c:c + 1], ALU.mult, [t_pTb, t_g], [t_B])
            ts(P, 'vector', kdec[:], pTb[:, 64:128], ekd[:, c:c + 1], ALU.mult, [t_pTb, t_g], [t_kd])
            mm(P, pTU, RT[:], Bm[:], True, True, [t_RT, t_B], [t_pTU])
            ts(P, 'vector', UW[i][:], pTU, Gd[:, c, 1:2], ALU.mult, [t_pTU, t_g], [t_UW[i]])
            if GPF < 5:
                continue
            U_, W_ = UW[i][:, 0:64], UW[i][:, 64:128]
            mm(P, pGH[:, 0:64], W_, kdec[:], True, True, [t_UW[i], t_kd], [t_pGH])
            mm(P, pGH[:, 64:128], kdec[:], U_, True, True, [t_UW[i], t_kd], [t_pGH])
            stt(P, Gt[:], C.f[0:64, 0, 0:64], egl[:, c:c + 1], pGH[:, 0:64], ALU.mult, ALU.subtract, [t_pGH, t_g, C.t], [t_GH])
            cp(P, 'scalar', Hm[:], pGH[:, 64:128], [t_pGH], [t_GH])
            mm(P, pGH, W_, AT[i][:], True, True, [t_UW[i], t_AT[i]], [t_pGH])
            tt(P, 'gpsimd', qg[:], kq2[:, 1, csl], eg64[:], ALU.mult, [t_kq, t_eg64], [t_qg])
            tt(P, 'vector', Qe[:], qg[:], pGH, ALU.subtract, [t_qg, t_pGH], [t_Qe])
            if GPF < 6:
                continue
            mm(P, pOS[:, 0:128], Sst[i][:], Qe[:], True, False, [t_S[i], t_Qe], [t_pOS])
            mm(P, pOS[:, 0:128], U_, AT[i][:], False, True, [t_UW[i], t_AT[i]], [t_pOS])
            mm(P, pOS[:, 128:192], Gt[:], Sst[i][:], True, True, [t_GH, t_S[i]], [t_pOS])
            if dr == 0:
                cp(P, 'scalar', oT[:, csl], pOS[:, 0:128], [t_pOS], [t_o])
            else:
                ov = oT[:, rsl(c)]
                tt(P, 'vector', ov, ov, pOS[:, 0:128], ALU.add, [t_pOS, t_o], [t_o])
            tt(P, 'vector', Sst[1 - i][:], Hm[:], pOS[:, 128:192], ALU.add, [t_pOS, t_GH], [t_S[1 - i]])
        P.pop()
        P.pop()
    emit_norm_gate(P, C, oT, t_o, projT, 832, None, cpar[0:64, 14:15], t_cp, yT, 192)
    P.pop()


def emit_proj(P, C, x_d, w_d, nw_d, projT):
    P.push()
    wb = P.sb([128, 8, NPROJ], BF16, 'wb'); t_wb = Tok()
    nw = P.sb([128, 8], F32, 'nw'); t_nw = Tok()
    nwb = P.sb([128, 8, 128], BF16, 'nwb')
    P.dma('sync', nw[:], nw_d, writes=[t_nw])
    for kc in range(8):
        ts(P, 'vector', nwb[:, kc, :], C.ones_f[:], nw[:, kc:kc + 1], ALU.mult, [t_nw, C.t], [t_nw])
    P.push()
    wst = [P.sb([128, NPROJ], F32, 'wst') for _ in range(2)]; t_wst = [Tok(), Tok()]
    for kc in range(8):
        i = kc % 2
        P.dma('sync', wst[i][:], w_d[128 * kc:128 * (kc + 1), :], writes=[t_wst[i]])
        cp(P, 'gpsimd' if kc % 2 else 'vector', wb[:, kc, :], wst[i][:], [t_wst[i]], [t_wb])
    P.pop()
    xt = [P.sb([128, 1024], F32, 'xt') for _ in range(3)]; t_xt = [Tok() for _ in range(3)]
    junk = P.sb([128, 1024], BF16, 'junk'); t_junk = Tok()
    ss = [P.sb([128, 1], F32, 'ss') for _ in range(3)]; t_ss = [Tok() for _ in range(3)]
    xs = [P.sb([128, 1024], BF16, 'xs') for _ in range(2)]; t_xs = [Tok(), Tok()]
    pT = [P.ps([128, 8, 128], BF16, 'pT') for _ in range(2)]; t_pT = [PTok(), PTok()]
    hn = [P.sb([128, 8, 512], BF16, 'hn') for _ in range(2)]; t_hn = [Tok(), Tok()]
    pm = [P.ps([128, 512], F32, 'pm') for _ in range(4)]; t_pm = [PTok() for _ in range(4)]
    ob = [P.sb([128, 512], F32, 'ob') for _ in range(4)]; t_ob = [Tok() for _ in range(4)]
    nt = 0
    ne = 0
    for tg in range(S // 512):
        hi = tg % 2
        for k in range(4):
            i = nt % 3
            j = nt % 2
            nt += 1
            r0 = tg * 512 + k * 128
            P.dma('sync', xt[i][:], x_d[r0:r0 + 128, :], writes=[t_xt[i]])
            act(P, junk[:], xt[i][:], AF.Square, [t_xt[i]], [t_junk, t_ss[i]], accum_out=ss[i][:])
            act(P, ss[i][:], ss[i][:], AF.Ln, [t_ss[i]], [t_ss[i]], scale=1.0 / 1024, bias=1e-6)
            act(P, ss[i][:], ss[i][:], AF.Exp, [t_ss[i]], [t_ss[i]], scale=-0.5)
            ts(P, 'vector', xs[j][:], xt[i][:], ss[i][:, 0:1], ALU.mult, [t_xt[i], t_ss[i]], [t_xs[j]])
            for kc in range(8):
                tr(P, pT[j][:, kc, :], xs[j][:, 128 * kc:128 * (kc + 1)], C.identb(), [t_xs[j], C.t], [t_pT[j]])
            tt(P, 'vector', hn[hi][:, :, 128 * k:128 * (k + 1)], pT[j], nwb[:], ALU.mult, [t_pT[j], t_nw], [t_hn[hi]])
        for ct in range(9):
            c0 = 128 * ct
            ncol = min(128, NPROJ - c0)
            e = ne % 4
            ne += 1
            for kc in range(8):
                mm(P, pm[e][0:ncol, :], wb[:, kc, c0:c0 + ncol], hn[hi][:, kc, :], kc == 0, kc == 7, [t_wb, t_hn[hi]], [t_pm[e]])
            cp(P, 'scalar' if ct % 2 else 'vector', ob[e][0:ncol, :], pm[e][0:ncol, :], [t_pm[e]], [t_ob[e]])
            P.dma('gpsimd' if ct % 2 else 'sync', projT[c0:c0 + ncol, tg * 512:(tg + 1) * 512], ob[e][0:ncol, :], reads=[t_ob[e]])
    P.pop()


def emit_outproj(P, C, yT_d, x_d, wo_d, out_d, ntok):
    P.push()
    wb = P.sb([128, 8, 1024], BF16, 'wob'); t_wb = Tok()
    P.push()
    wst = [P.sb([128, 1024], F32, 'wost') for _ in range(2)]; t_wst = [Tok(), Tok()]
    for kc in range(8):
        i = kc % 2
        P.dma('sync', wst[i][:], wo_d[128 * kc:128 * (kc + 1), :], writes=[t_wst[i]])
        cp(P, 'gpsimd' if kc % 2 else 'vector', wb[:, kc, :], wst[i][:], [t_wst[i]], [t_wb])
    P.pop()
    yf = [P.sb([128, 8, 128], F32, 'yf') for _ in range(2)]; t_yf = [Tok(), Tok()]
    yb = [P.sb([128, 8, 128], BF16, 'yb') for _ in range(2)]; t_yb = [Tok(), Tok()]
    xt = [P.sb([128, 1024], F32, 'xto') for _ in range(2)]; t_xt = [Tok(), Tok()]
    po = [P.ps([128, 2, 512], F32, 'po') for _ in range(2)]; t_po = [PTok(), PTok()]
    for t in range(ntok // 128):
        i = t % 2
        tsl = slice(128 * t, 128 * (t + 1))
        P.dma('sync', yf[i][:], yT_d[:, tsl].rearrange("(c p) t -> p c t", p=128), writes=[t_yf[i]])
        P.dma('gpsimd', xt[i][:], x_d[tsl, :], writes=[t_xt[i]])
        cp(P, 'gpsimd', yb[i][:], yf[i][:], [t_yf[i]], [t_yb[i]])
        for n in range(2):
            for kc in range(8):
                mm(P, po[i][:, n, :], yb[i][:, kc, :], wb[:, kc, 512 * n:512 * (n + 1)], kc == 0, kc == 7, [t_yb[i], t_wb], [t_po[i]])
        tt(P, 'vector', xt[i][:], xt[i][:], po[i].rearrange("p n c -> p (n c)"), ALU.add, [t_xt[i], t_po[i]], [t_xt[i]])
        P.dma('sync', out_d[tsl, :], xt[i][:], reads=[t_xt[i]])
    P.pop()


def core_w_in_cols(g):
    cols = []
    for off in (0, 512, 1024, 1536):
        cols += list(range(off + 128 * g, off + 128 * g + 128))
    for off in (2048, 2304, 2560, 2832, 3088):
        cols += list(range(off + 64 * g, off + 64 * g + 64))
    cols += list(range(4128 + 64 * g, 4128 + 64 * g + 64))
    for off in (3344, 3600, 3856):
        cols += list(range(off + 64 * g, off + 64 * g + 64))
    cols += [2816 + 4 * k + g for k in range(4)]
    cols += [4112 + 4 * k + g for k in range(4)]
    assert len(cols) == NPROJ
    return np.array(cols)


def core_params(inp, l, g):
    qkw = np.ascontiguousarray(np.tile(inp['qk_norm_w'][l].T, (2, 1))).astype(np.float32)
    bpar = np.zeros((128, 5), np.float32)
    bpar[:, 0] = inp['mlstm_f_bias'][l, 0, g]
    bpar[:, 1] = inp['mlstm_f_bias'][l, 1, g]
    bpar[:, 2] = inp['mlstm_i_bias'][l, 0, g]
    bpar[:, 3] = inp['mlstm_i_bias'][l, 1, g]
    bpar[:64, 4] = inp['mlstm_norm_w'][l]
    cw = inp['gdn_conv_w'][l]
    cpar = np.zeros((128, 16), np.float32)
    cpar[0:64, 0:5] = cw[:, 64 * g:64 * g + 64].T
    cpar[64:128, 0:5] = cw[:, 256 + 64 * g:256 + 64 * g + 64].T
    cpar[0:64, 5:10] = cw[:, 512 + 64 * g:512 + 64 * g + 64].T
    cpar[:, 10] = inp['gdn_a_log'][l, 0, g]
    cpar[:, 11] = inp['gdn_a_log'][l, 1, g]
    cpar[:, 12] = inp['gdn_dt_bias'][l, 0, g]
    cpar[:, 13] = inp['gdn_dt_bias'][l, 1, g]
    cpar[0:64, 14] = inp['gdn_norm_w'][l]
    nw = np.ascontiguousarray(inp['norm_w'][l].reshape(8, 128).T).astype(np.float32)
    return dict(qkw=qkw, bpar=bpar, cpar=cpar, nw=nw,
                maskb=attn_mask_tables(inp['rel_bias'], g),
                w=np.ascontiguousarray(inp['w_in'][l][:, core_w_in_cols(g)]).astype(np.float32))


_NC_CACHE = {}


def build_mix():
    nc = bass.Bass("TRN2", target_bir_lowering=False)
    d = lambda name, shape, kind="ExternalInput": nc.dram_tensor(name, shape, F32, kind=kind).ap()
    x_d = d("x", [S, 1024]); w_d = d("w", [1024, NPROJ]); nw_d = d("nw", [128, 8]); cpk_d = d("cpk", [128, 6, 128])
    qkw_d = d("qkw", [128, 2]); maskb_d = d("maskb", [128, 1536]); bpar_d = d("bpar", [128, 5]); cpar_d = d("cpar", [128, 16])
    yT_d = d("yT", [256, S], "ExternalOutput")
    projT = d("projT", [NPROJ, S], "Internal")
    P = Prog(nc)
    C = Consts(P, cpk_d)
    emit_proj(P, C, x_d, w_d, nw_d, projT)
    emit_attn(P, C, projT, qkw_d, maskb_d, yT_d)
    emit_mlstm(P, C, projT, bpar_d, yT_d)
    emit_gdn(P, C, projT, cpar_d, yT_d)
    P.finish()
    return nc


def build_outproj(ntok):
    nc = bass.Bass("TRN2", target_bir_lowering=False)
    d = lambda name, shape, kind="ExternalInput": nc.dram_tensor(name, shape, F32, kind=kind).ap()
    yT_d = d("yT", [1024, ntok]); x_d = d("x", [ntok, 1024]); wo_d = d("wo", [1024, 1024]); cpk_d = d("cpk", [128, 6, 128])
    out_d = d("out", [ntok, 1024], "ExternalOutput")
    P = Prog(nc)
    C = Consts(P, cpk_d)
    emit_outproj(P, C, yT_d, x_d, wo_d, out_d, ntok)
    P.finish()
    return nc


def kernel(**inp):
    inp = {k: np.asarray(v) for k, v in inp.items()}
    x = np.ascontiguousarray(inp['x']).astype(np.float32)
    B = x.shape[0]
    cpk = const_pack()
    if 'mix' not in _NC_CACHE:
        _NC_CACHE['mix'] = build_mix()
        _NC_CACHE['op'] = build_outproj(S * B // 8)
    for l in range(2):
        in_maps = []
        for c in range(8):
            b, g = c // 4, c % 4
            pr = core_params(inp, l, g)
            in_maps.append(dict(x=x[b], w=pr['w'], nw=pr['nw'], cpk=cpk, qkw=pr['qkw'], maskb=pr['maskb'],
                                bpar=pr['bpar'], cpar=pr['cpar']))
        res = run_bass_kernel_spmd(_NC_CACHE['mix'], in_maps, core_ids=list(range(8)))
        yT = np.zeros((B, 1024, S), np.float32)
        for c in range(8):
            b, g = c // 4, c % 4
            r = res.results[c]['yT']
            yT[b, 128 * g:128 * g + 128] = r[0:128]
            yT[b, 512 + 64 * g:512 + 64 * g + 64] = r[128:192]
            yT[b, 768 + 64 * g:768 + 64 * g + 64] = r[192:256]
        nt = S // 4
        in_maps = []
        for c in range(8):
            b, q = c // 4, c % 4
            in_maps.append(dict(yT=np.ascontiguousarray(yT[b][:, q * nt:(q + 1) * nt]), x=np.ascontiguousarray(x[b, q * nt:(q + 1) * nt]),
                                wo=np.ascontiguousarray(inp['w_out'][l]).astype(np.float32), cpk=cpk))
        res = run_bass_kernel_spmd(_NC_CACHE['op'], in_maps, core_ids=list(range(8)))
        xn = np.zeros_like(x)
        for c in range(8):
            b, q = c // 4, c % 4
            xn[b, q * nt:(q + 1) * nt] = res.results[c]['out']
        x = xn
    return x
```
